# Optimizing a Trainium2 kernel written in Bass

```python
import math
import jax, jax.numpy as jnp
from jax import lax
import numpy as np

D_MODEL = 4096
BATCH = 2
SEQ = 4096
DEPTH = 2

MIX_WIDTH = 2 * D_MODEL
SSD_WIDTH = MIX_WIDTH // 2
LRU_WIDTH = MIX_WIDTH // 4
MLSTM_WIDTH = MIX_WIDTH - SSD_WIDTH - LRU_WIDTH
SSD_HEAD_DIM = 64
SSD_HEADS = SSD_WIDTH // SSD_HEAD_DIM
SSD_GROUPS = 8
SSD_HEADS_PER_GROUP = SSD_HEADS // SSD_GROUPS
SSD_STATE = 128
SSD_CONV_DIM = SSD_WIDTH + 2 * SSD_GROUPS * SSD_STATE
LRU_BLOCKS = 16
LRU_BLOCK_DIM = LRU_WIDTH // LRU_BLOCKS
LRU_C = 8.0
MLSTM_HEADS = 8
MLSTM_HEAD_DIM = MLSTM_WIDTH // MLSTM_HEADS
CONV_WIDTH = 4
CHUNK = 128
DEEPNORM_ALPHA = (2.0 * DEPTH) ** 0.25
DEEPNORM_BETA = (8.0 * DEPTH) ** -0.25
IN_SIZES = (SSD_WIDTH, SSD_CONV_DIM, SSD_HEADS, LRU_WIDTH, LRU_WIDTH,
            MLSTM_WIDTH, MLSTM_WIDTH, MLSTM_WIDTH, 2 * MLSTM_HEADS)
N_IN = sum(IN_SIZES)

kernel_name = "hymba_style_ssd_rglru_mlstm_deepnorm"

F32 = jnp.float32


def _split_points():
    pts, acc = [], 0
    for s in IN_SIZES[:-1]:
        acc += s
        pts.append(acc)
    return pts


def causal_conv(x, w, b):
    c = x.shape[-1]
    y = lax.conv_general_dilated(x, w[:, None, :].astype(x.dtype), window_strides=(1,),
                                 padding=[(CONV_WIDTH - 1, 0)],
                                 dimension_numbers=("NWC", "WIO", "NWC"),
                                 feature_group_count=c)
    return y + b


def rms_norm_last(x, eps=1e-6):
    x = x.astype(F32)
    return x * lax.rsqrt(jnp.mean(x * x, axis=-1, keepdims=True) + eps)


def layer_norm(x, w, b, eps=1e-5):
    x = x.astype(F32)
    mu = jnp.mean(x, axis=-1, keepdims=True)
    xc = x - mu
    var = jnp.mean(xc * xc, axis=-1, keepdims=True)
    return xc * lax.rsqrt(var + eps) * w + b


def ssd_mixer(xbc, dt_raw, z, conv_w, conv_b, dt_bias, a_log, d_skip, norm_w):
    bsz, seq, _ = xbc.shape
    nc = seq // CHUNK
    G, R, P, N = SSD_GROUPS, SSD_HEADS_PER_GROUP, SSD_HEAD_DIM, SSD_STATE
    xbc = jax.nn.silu(causal_conv(xbc, conv_w.astype(F32), conv_b.astype(F32)))
    xs, bm, cm = jnp.split(xbc, [SSD_WIDTH, SSD_WIDTH + G * N], axis=-1)
    xs = xs.reshape(bsz, nc, CHUNK, G, R, P)
    bm = bm.reshape(bsz, nc, CHUNK, G, N)
    cm = cm.reshape(bsz, nc, CHUNK, G, N)
    dt = jax.nn.softplus(dt_raw + dt_bias.astype(F32)).reshape(bsz, nc, CHUNK, G, R)
    a_neg = -jnp.exp(a_log.astype(F32)).reshape(G, R)
    dt_t = jnp.moveaxis(dt, 2, -1)
    a_cum = jnp.cumsum(dt_t * a_neg[:, :, None], axis=-1)
    causal = jnp.tril(jnp.ones((CHUNK, CHUNK), bool))
    seg = a_cum[..., :, None] - a_cum[..., None, :]
    decay = jnp.exp(jnp.where(causal, seg, -jnp.inf))
    cb = jnp.einsum("bclgn,bcsgn->bcgls", cm, bm)
    wts = cb[:, :, :, None] * decay * dt_t[..., None, :]
    y_diag = jnp.einsum("bcgrls,bcsgrp->bclgrp", wts, xs)
    decay_states = jnp.exp(a_cum[..., -1:] - a_cum) * dt_t
    states = jnp.einsum("bclgn,bcgrl,bclgrp->bcgrpn", bm, decay_states, xs)
    chunk_decay = jnp.exp(a_cum[..., -1])

    def step(h, inp):
        s, dcy = inp
        return h * dcy[..., None, None] + s, h

    h0 = jnp.zeros((bsz, G, R, P, N), F32)
    _, prev = lax.scan(step, h0, (jnp.moveaxis(states, 1, 0), jnp.moveaxis(chunk_decay, 1, 0)))
    prev = jnp.moveaxis(prev, 0, 1)
    y_off = jnp.einsum("bclgn,bcgrpn->bclgrp", cm, prev) * jnp.moveaxis(jnp.exp(a_cum), -1, 2)[..., None]
    y = y_diag + y_off + d_skip.astype(F32).reshape(G, R)[:, :, None] * xs
    y = y.reshape(bsz, seq, SSD_WIDTH) * jax.nn.silu(z)
    y = rms_norm_last(y.reshape(bsz, seq, G, SSD_WIDTH // G)).reshape(bsz, seq, SSD_WIDTH)
    return y * norm_w


def _lin_combine(left, right):
    a1, b1 = left
    a2, b2 = right
    return a1 * a2, a2 * b1 + b2


def rglru_mixer(xl, gate, conv_w, conv_b, w_a, b_a, w_x, b_x, lam):
    bsz, seq, _ = xl.shape
    xc = causal_conv(xl, conv_w.astype(F32), conv_b.astype(F32))
    xb = xc.reshape(bsz, seq, LRU_BLOCKS, LRU_BLOCK_DIM)
    r = jax.nn.sigmoid(jnp.einsum("blhi,hij->blhj", xb, w_a.astype(F32)).reshape(bsz, seq, LRU_WIDTH) + b_a)
    i = jax.nn.sigmoid(jnp.einsum("blhi,hij->blhj", xb, w_x.astype(F32)).reshape(bsz, seq, LRU_WIDTH) + b_x)
    log_a = -LRU_C * r * jax.nn.softplus(-lam.astype(F32))
    a = jnp.exp(log_a)
    u = jnp.sqrt(-jnp.expm1(2.0 * log_a)) * (i * xc)
    _, h = lax.associative_scan(_lin_combine, (a, u), axis=1)
    return h * jax.nn.silu(gate)


def mlstm_mixer(xm, o_pre, gate, if_pre, conv_w, conv_b, w_q, w_k, w_v, b_i, b_f, norm_w):
    bsz, seq, _ = xm.shape
    nc = seq // CHUNK
    H, Dh = MLSTM_HEADS, MLSTM_HEAD_DIM
    xc = jax.nn.silu(causal_conv(xm, conv_w.astype(F32), conv_b.astype(F32))).reshape(bsz, seq, H, Dh)
    q = jnp.einsum("blhi,hij->bhlj", xc, w_q.astype(F32))
    k = jnp.einsum("blhi,hij->bhlj", xc, w_k.astype(F32)) * (Dh ** -0.5)
    v = jnp.einsum("blhi,hij->bhlj", xm.reshape(bsz, seq, H, Dh), w_v.astype(F32))
    i_pre, f_pre = jnp.split(if_pre, 2, axis=-1)
    i_pre = jnp.moveaxis(i_pre + b_i, -1, 1)
    log_f = jax.nn.log_sigmoid(jnp.moveaxis(f_pre + b_f, -1, 1))

    def to_chunks(t):
        t = t.reshape(t.shape[:2] + (nc, CHUNK) + t.shape[3:])
        return jnp.moveaxis(t, 2, 0)

    causal = jnp.tril(jnp.ones((CHUNK, CHUNK), bool))

    def body(carry, inp):
        c_st, n_st, m_st = carry
        qc, kc, vc, ic, lfc = inp
        bcum = jnp.cumsum(lfc, axis=-1)
        a_inter = bcum + m_st[..., None]
        d_intra = jnp.where(causal, bcum[..., :, None] - bcum[..., None, :] + ic[..., None, :], -jnp.inf)
        m_t = jnp.maximum(a_inter, jnp.max(d_intra, axis=-1))
        w_intra = jnp.exp(d_intra - m_t[..., None])
        w_inter = jnp.exp(a_inter - m_t)
        s = jnp.einsum("bhtd,bhsd->bhts", qc, kc) * w_intra
        num = jnp.einsum("bhts,bhse->bhte", s, vc) + w_inter[..., None] * jnp.einsum("bhtd,bhde->bhte", qc, c_st)
        den = jnp.sum(s, axis=-1) + w_inter * jnp.einsum("bhtd,bhd->bht", qc, n_st)
        h = num / jnp.maximum(jnp.abs(den), jnp.exp(-m_t))[..., None]
        m_new = m_t[..., -1]
        w_s = jnp.exp(bcum[..., -1:] - bcum + ic - m_new[..., None])
        carry_decay = jnp.exp(bcum[..., -1] + m_st - m_new)
        c_new = carry_decay[..., None, None] * c_st + jnp.einsum("bhs,bhsd,bhse->bhde", w_s, kc, vc)
        n_new = carry_decay[..., None] * n_st + jnp.einsum("bhs,bhsd->bhd", w_s, kc)
        return (c_new, n_new, m_new), h

    init = (jnp.zeros((bsz, H, Dh, Dh), F32), jnp.zeros((bsz, H, Dh), F32), jnp.zeros((bsz, H), F32))
    _, hs = lax.scan(body, init, (to_chunks(q), to_chunks(k), to_chunks(v), to_chunks(i_pre), to_chunks(log_f)))
    hs = jnp.transpose(hs, (1, 0, 3, 2, 4)).reshape(bsz, seq, H, Dh)
    hs = (rms_norm_last(hs) * norm_w.astype(F32).reshape(H, Dh)).reshape(bsz, seq, MLSTM_WIDTH)
    return hs * jax.nn.sigmoid(o_pre) * jax.nn.silu(gate)


def hybrid_layer(x, w_in, ssd_conv_w, ssd_conv_b, ssd_dt_bias, ssd_a_log, ssd_d, ssd_norm_w,
                 lru_conv_w, lru_conv_b, lru_w_a, lru_b_a, lru_w_x, lru_b_x, lru_lambda,
                 mlstm_conv_w, mlstm_conv_b, mlstm_w_q, mlstm_w_k, mlstm_w_v, mlstm_b_i, mlstm_b_f,
                 mlstm_norm_w, w_out, ln_w, ln_b):
    proj = jnp.einsum("bld,dn->bln", x, w_in).astype(F32)
    (ssd_z, ssd_xbc, ssd_dt, lru_x, lru_gate,
     ml_x, ml_o, ml_gate, ml_if) = jnp.split(proj, _split_points(), axis=-1)
    y_ssd = ssd_mixer(ssd_xbc, ssd_dt, ssd_z, ssd_conv_w, ssd_conv_b, ssd_dt_bias, ssd_a_log, ssd_d, ssd_norm_w)
    y_lru = rglru_mixer(lru_x, lru_gate, lru_conv_w, lru_conv_b, lru_w_a, lru_b_a, lru_w_x, lru_b_x, lru_lambda)
    y_ml = mlstm_mixer(ml_x, ml_o, ml_gate, ml_if, mlstm_conv_w, mlstm_conv_b, mlstm_w_q, mlstm_w_k,
                       mlstm_w_v, mlstm_b_i, mlstm_b_f, mlstm_norm_w)
    y = jnp.concatenate([y_ssd, y_lru, y_ml], axis=-1).astype(x.dtype)
    out = jnp.einsum("bln,nd->bld", y, w_out)
    return layer_norm(DEEPNORM_ALPHA * x.astype(F32) + out.astype(F32), ln_w, ln_b).astype(x.dtype)


def setup_inputs(seed: int = 0) -> dict:
    key = jax.random.key(seed)
    ks = jax.random.split(key, 26)

    def nrm(k, shape, s):
        return s * jax.random.normal(k, shape, F32)

    x = nrm(ks[0], (BATCH, SEQ, D_MODEL), 1.0)
    w_in = nrm(ks[1], (DEPTH, D_MODEL, N_IN), D_MODEL ** -0.5)
    ssd_conv_w = nrm(ks[2], (DEPTH, CONV_WIDTH, SSD_CONV_DIM), CONV_WIDTH ** -0.5)
    ssd_conv_b = nrm(ks[3], (DEPTH, SSD_CONV_DIM), 0.01)
    dt = jnp.exp(jax.random.uniform(ks[4], (DEPTH, SSD_HEADS), F32, math.log(1e-3), math.log(1e-1)))
    ssd_dt_bias = dt + jnp.log(-jnp.expm1(-dt))
    ssd_a_log = jnp.log(jax.random.uniform(ks[5], (DEPTH, SSD_HEADS), F32, 1.0, 16.0))
    ssd_d = 1.0 + nrm(ks[6], (DEPTH, SSD_HEADS), 0.01)
    ssd_norm_w = 1.0 + nrm(ks[7], (DEPTH, SSD_WIDTH), 0.01)
    lru_conv_w = nrm(ks[8], (DEPTH, CONV_WIDTH, LRU_WIDTH), CONV_WIDTH ** -0.5)
    lru_conv_b = nrm(ks[9], (DEPTH, LRU_WIDTH), 0.01)
    lru_w_a = nrm(ks[10], (DEPTH, LRU_BLOCKS, LRU_BLOCK_DIM, LRU_BLOCK_DIM), LRU_BLOCK_DIM ** -0.5)
    lru_b_a = nrm(ks[11], (DEPTH, LRU_WIDTH), 0.01)
    lru_w_x = nrm(ks[12], (DEPTH, LRU_BLOCKS, LRU_BLOCK_DIM, LRU_BLOCK_DIM), LRU_BLOCK_DIM ** -0.5)
    lru_b_x = nrm(ks[13], (DEPTH, LRU_WIDTH), 0.01)
    a_c = jax.random.uniform(ks[14], (DEPTH, LRU_WIDTH), F32, 0.9, 0.999)
    sig = a_c ** (1.0 / LRU_C)
    lru_lambda = jnp.log(sig) - jnp.log1p(-sig)
    mlstm_conv_w = nrm(ks[15], (DEPTH, CONV_WIDTH, MLSTM_WIDTH), CONV_WIDTH ** -0.5)
    mlstm_conv_b = nrm(ks[16], (DEPTH, MLSTM_WIDTH), 0.01)
    mlstm_w_q = nrm(ks[17], (DEPTH, MLSTM_HEADS, MLSTM_HEAD_DIM, MLSTM_HEAD_DIM), MLSTM_HEAD_DIM ** -0.5)
    mlstm_w_k = nrm(ks[18], (DEPTH, MLSTM_HEADS, MLSTM_HEAD_DIM, MLSTM_HEAD_DIM), MLSTM_HEAD_DIM ** -0.5)
    mlstm_w_v = nrm(ks[19], (DEPTH, MLSTM_HEADS, MLSTM_HEAD_DIM, MLSTM_HEAD_DIM), MLSTM_HEAD_DIM ** -0.5)
    mlstm_b_i = nrm(ks[20], (DEPTH, MLSTM_HEADS), 0.1)
    mlstm_b_f = jnp.linspace(3.0, 6.0, MLSTM_HEADS, dtype=F32)[None, :] + nrm(ks[21], (DEPTH, MLSTM_HEADS), 0.01)
    mlstm_norm_w = 1.0 + nrm(ks[22], (DEPTH, MLSTM_WIDTH), 0.01)
    w_out = nrm(ks[23], (DEPTH, MIX_WIDTH, D_MODEL), (MIX_WIDTH ** -0.5) * DEEPNORM_BETA)
    ln_w = 1.0 + nrm(ks[24], (DEPTH, D_MODEL), 0.01)
    ln_b = nrm(ks[25], (DEPTH, D_MODEL), 0.01)
    return {"x": x, "w_in": w_in, "ssd_conv_w": ssd_conv_w, "ssd_conv_b": ssd_conv_b,
            "ssd_dt_bias": ssd_dt_bias, "ssd_a_log": ssd_a_log, "ssd_d": ssd_d, "ssd_norm_w": ssd_norm_w,
            "lru_conv_w": lru_conv_w, "lru_conv_b": lru_conv_b, "lru_w_a": lru_w_a, "lru_b_a": lru_b_a,
            "lru_w_x": lru_w_x, "lru_b_x": lru_b_x, "lru_lambda": lru_lambda,
            "mlstm_conv_w": mlstm_conv_w, "mlstm_conv_b": mlstm_conv_b, "mlstm_w_q": mlstm_w_q,
            "mlstm_w_k": mlstm_w_k, "mlstm_w_v": mlstm_w_v, "mlstm_b_i": mlstm_b_i, "mlstm_b_f": mlstm_b_f,
            "mlstm_norm_w": mlstm_norm_w, "w_out": w_out, "ln_w": ln_w, "ln_b": ln_b}


def reference(x, w_in, ssd_conv_w, ssd_conv_b, ssd_dt_bias, ssd_a_log, ssd_d, ssd_norm_w,
              lru_conv_w, lru_conv_b, lru_w_a, lru_b_a, lru_w_x, lru_b_x, lru_lambda,
              mlstm_conv_w, mlstm_conv_b, mlstm_w_q, mlstm_w_k, mlstm_w_v, mlstm_b_i, mlstm_b_f,
              mlstm_norm_w, w_out, ln_w, ln_b):
    for l in range(DEPTH):
        x = hybrid_layer(x, w_in[l], ssd_conv_w[l], ssd_conv_b[l], ssd_dt_bias[l], ssd_a_log[l], ssd_d[l],
                         ssd_norm_w[l], lru_conv_w[l], lru_conv_b[l], lru_w_a[l], lru_b_a[l], lru_w_x[l],
                         lru_b_x[l], lru_lambda[l], mlstm_conv_w[l], mlstm_conv_b[l], mlstm_w_q[l],
                         mlstm_w_k[l], mlstm_w_v[l], mlstm_b_i[l], mlstm_b_f[l], mlstm_norm_w[l],
                         w_out[l], ln_w[l], ln_b[l])
    return x
```

```python
import numpy as np
import concourse.bass as bass
import concourse.mybir as mybir
from contextlib import ExitStack

F32 = mybir.dt.float32
BF16 = mybir.dt.bfloat16
AF = mybir.ActivationFunctionType
ALU = mybir.AluOpType
AX = mybir.AxisListType

ENGS = ("pe", "act", "dve", "pool", "sp")


class Tile:
    def __init__(self, h, name):
        self.h = h
        self.name = name
        self.last_w = None
        self.readers = {}

    def __getitem__(self, k):
        return self.h[k]


class Op:
    __slots__ = ("eng", "fn", "deps", "dma_sem", "dma_val", "need_inc", "val", "is_dma")

    def __init__(self, eng, fn, is_dma=False):
        self.eng = eng
        self.fn = fn
        self.deps = []
        self.is_dma = is_dma
        self.dma_sem = None
        self.dma_val = 0
        self.need_inc = False
        self.val = 0


class DmaSem:
    def __init__(self, sem, key):
        self.sem = sem
        self.count = 0
        self.key = key


class Prog:
    def __init__(self, nc, stack):
        self.nc = nc
        self.stack = stack
        self.streams = {e: [] for e in ENGS}
        self.esem = {}
        for e in ENGS:
            self.esem[e] = stack.enter_context(nc.semaphore("s_" + e))
        self.ndsem = 0
        self.ntile = 0

    def sbuf(self, shape, dtype, name=None):
        self.ntile += 1
        name = "sb_" + (name or "t%d" % self.ntile)
        h = self.stack.enter_context(self.nc.sbuf_tensor(name, list(shape), dtype))
        return Tile(h, name)

    def psum(self, shape, dtype, name=None):
        self.ntile += 1
        name = "ps_" + (name or "p%d" % self.ntile)
        h = self.stack.enter_context(self.nc.psum_tensor(name, list(shape), dtype))
        return Tile(h, name)

    def dram(self, name, shape, dtype, kind="Internal"):
        t = self.nc.dram_tensor(name, list(shape), dtype, kind=kind)
        return Tile(t.ap(), name)

    def dsem(self):
        self.ndsem += 1
        s = self.stack.enter_context(self.nc.semaphore("d%d" % self.ndsem))
        return DmaSem(s, "d%d" % self.ndsem)

    def _track(self, op, reads, writes):
        def same(d):
            return (not d.is_dma) and (not op.is_dma) and d.eng == op.eng
        for r in reads:
            if r.last_w is not None:
                op.deps.append(r.last_w)
        for w in writes:
            if w.last_w is not None and not same(w.last_w):
                op.deps.append(w.last_w)
            for rd in w.readers.values():
                if not same(rd):
                    op.deps.append(rd)
        for r in reads:
            key = op.dma_sem.key if op.is_dma else op.eng
            r.readers[key] = op
        for w in writes:
            w.last_w = op
            w.readers = {}

    def op(self, eng, fn, reads=(), writes=()):
        o = Op(eng, fn)
        self._track(o, reads, writes)
        self.streams[eng].append(o)
        return o

    def dma(self, eng, out, in_, dsem, reads=(), writes=(), **kw):
        o = Op(eng, lambda e: e.dma_start(out=out, in_=in_, **kw), is_dma=True)
        o.dma_sem = dsem
        dsem.count += 16
        o.dma_val = dsem.count
        self._track(o, reads, writes)
        self.streams[eng].append(o)
        return o

    def emit(self, final_waits=()):
        for e in ENGS:
            for o in self.streams[e]:
                for d in o.deps:
                    if not d.is_dma:
                        if d.eng == o.eng and d.eng in ("pe", "sp"):
                            continue
                        d.need_inc = True
        for e in ENGS:
            c = 0
            for o in self.streams[e]:
                if o.need_inc and not o.is_dma:
                    c += 1
                    o.val = c
        nc = self.nc
        prog = self

        def run(ename, eng):
            seen = {}
            for o in prog.streams[ename]:
                need = {}
                for d in o.deps:
                    if d.is_dma:
                        k, s, v = d.dma_sem.key, d.dma_sem.sem, d.dma_val
                    else:
                        if d.eng == ename and ename in ("pe", "sp"):
                            continue
                        k, s, v = d.eng, prog.esem[d.eng], d.val
                    if v > need.get(k, (None, 0))[1]:
                        need[k] = (s, v)
                for k, (s, v) in need.items():
                    if seen.get(k, 0) >= v:
                        continue
                    eng.wait_ge(s, v)
                    seen[k] = v
                ins = o.fn(eng)
                if o.is_dma:
                    ins.then_inc(o.dma_sem.sem, 16)
                elif o.need_inc:
                    ins.then_inc(prog.esem[ename], 1)
            if ename == "sp":
                for ds in final_waits:
                    eng.wait_ge(ds.sem, ds.count)

        with nc.Block() as block:
            @block.sync
            def _(e):
                run("sp", e)

            @block.tensor
            def _(e):
                run("pe", e)

            @block.scalar
            def _(e):
                run("act", e)

            @block.vector
            def _(e):
                run("dve", e)

            @block.gpsimd
            def _(e):
                run("pool", e)


D_MODEL = 4096
NDC = 32
TB = 512
NQ = 4
NGRP = 10
RMS_EPS = 1e-6
FM_GROUPS = [(6, 7), (8, 9), (0, 1), (2, 3), (4, 5), (10, 11)]
GORDER = [("fm", 0), ("fm", 1), ("fm", 2), ("fm", 3), ("fm", 4), ("tm", 0), ("tm", 1), ("sm", 0), ("fm", 5),
          ("tm", 2), ("tm", 3)]


NGTEST = 2


def build_a(NB, TPB, stage=9):
    nc = bass.Bass("TRN2", target_bir_lowering=False)
    NTOK = NB * TPB
    NBLK = TPB // TB

    def din(name, shape, dt=F32):
        return nc.dram_tensor(name, list(shape), dt, kind="ExternalInput").ap()

    xT_d = din("xT", [NTOK // TB, 128, NDC * TB])
    wg_d = din("wg", [NGRP, 128, NDC * 256])
    ws_d = din("ws", [128, NDC, 10])
    cst_d = din("cst", [128, 4, 128])
    cw_d = din("cw", [128, 12, 4])
    cb_d = din("cb", [128, 12])
    lw_d = din("lw", [128, 4, 128])
    lp_d = din("lp", [128, 6])
    mw_d = din("mw", [128, 6, 256])
    mp_d = din("mp", [128, 2])
    mn_d = din("mn", [128, 256])
    sn_d = din("sn", [128, 512])
    sp_d = din("sp", [128, 3, 8])
    ylru_d = nc.dram_tensor("ylru", [256, NTOK], BF16, kind="ExternalOutput").ap()
    yssd_d = nc.dram_tensor("yssd", [NTOK, 512], BF16, kind="ExternalOutput").ap()
    yml_d = nc.dram_tensor("yml", [NTOK, 256], BF16, kind="ExternalOutput").ap()
    wgb_d = nc.dram_tensor("wgb", [NGRP, NDC * 128, 256], BF16, kind="Internal").ap()

    with ExitStack() as st:
        P = Prog(nc, st)
        sb, ps = P.sbuf, P.psum
        xblk = sb([128, NDC, TB], BF16, "xblk")
        xsem = P.dsem()
        wsl = [sb([128, 16, 256], BF16, "wsl%d" % i) for i in range(3)]
        wsem = [P.dsem() for _ in range(3)]
        wsm = sb([128, NDC, 10], BF16, "wsm")
        cst = sb([128, 4, 128], F32, "cst")
        U, ONES, MNEG, IDN = (cst[:, i, :] for i in range(4))
        idb = sb([128, 128], BF16, "idb")
        cw = sb([128, 12, 4], F32, "cw")
        cb = sb([128, 12], F32, "cb")
        lw = sb([128, 4, 128], BF16, "lw")
        lp = sb([128, 6], F32, "lp")
        lq = sb([128, 8], F32, "lq")
        mw = sb([128, 6, 256], BF16, "mw")
        mp = sb([128, 2], F32, "mp")
        mn = sb([128, 256], F32, "mn")
        sn = sb([128, 512], F32, "sn")
        spp = sb([128, 3, 8], F32, "spp")
        aneg = sb([128, 8], F32, "aneg")
        dmat = sb([128, 8, 128], BF16, "dmat")
        csem = P.dsem()
        g0sem = P.dsem()
        FMp = [None if j in (8, 9) else sb([128, 3 + TB], F32, "fmp%d" % j) for j in range(12)]
        pj = [ps([128, 512], F32, "pj%d" % i) for i in range(4)]
        pS = ps([128, 512], F32, "pS")
        pm = [ps([128, 512], F32, "pm%d" % i) for i in range(3)]
        pmi = [0]

        def nextpm():
            t = pm[pmi[0] % 3]
            pmi[0] += 1
            return t
        l_xcb = sb([128, TB], BF16, "l_xcb")
        l_h = [sb([128, TB], F32, "l_h0")] * 2
        l_sg = [sb([128, TB], F32, "l_sg%d" % i) for i in range(2)]
        l_y = [sb([128, TB], BF16, "l_y%d" % i) for i in range(2)]
        l_ysem = [P.dsem() for _ in range(2)]
        l_hprev = sb([128, 2], F32, "l_hprev")
        s_t = [sb([128, TB], F32, "s_t%d" % i) for i in range(2)]
        s_th = [sb([128, TB], F32, "s_th%d" % i) for i in range(2)]
        l_xc, l_a, l_m, l_u = s_t[0], s_t[1], s_th[0], s_th[1]
        s_xsT = sb([128, 4, TB], BF16, "s_xsT")
        s_BT = sb([128, TB], BF16, "s_BT")
        s_CT = sb([128, TB], BF16, "s_CT")
        s_tok = [sb([128, 640], BF16, "s_tok%d" % i) for i in range(2)]
        s_xsc = [sb([128, 512], BF16, "s_xsc%d" % i) for i in range(2)]
        s_G = [sb([128, 128], F32, "s_G%d" % i) for i in range(2)]
        s_E = [sb([128, 128], F32, "s_E%d" % i) for i in range(4)]
        s_W = [sb([128, 128], BF16, "s_W%d" % i) for i in range(8)]
        s_S = sb([128, 512], F32, "s_S")
        s_Sb = sb([128, 512], BF16, "s_Sb")
        s_zs = sb([128, NQ, 512], F32, "s_zs")
        s_yg = s_zs
        s_y1 = [sb([128, 512], F32, "s_y1%d" % i) for i in range(2)]
        s_junk = sb([128, 512], F32, "s_junk")
        s_o = [sb([128, 512], BF16, "s_o%d" % i) for i in range(2)]
        s_osem = [P.dsem() for _ in range(2)]
        s_sm = sb([128, NQ, 10], F32, "s_sm")
        s_dt = sb([128, NQ, 8], F32, "s_dt")
        s_a = sb([128, NQ, 8], F32, "s_a")
        s_acum = sb([128, NQ, 8], F32, "s_acum")
        s_nacum = sb([128, NQ, 8], F32, "s_nacum")
        s_eacum = sb([128, NQ, 8], F32, "s_eacum")
        s_ecd = sb([128, NQ, 8], F32, "s_ecd")
        s_ds = sb([128, NQ, 8], F32, "s_ds")
        s_ssq = sb([128, NQ], F32, "s_ssq")
        s_rstd = sb([128, NQ], F32, "s_rstd")
        m_xcT = sb([128, 2, TB], BF16, "m_xcT")
        m_xmT = sb([128, 2, TB], BF16, "m_xmT")
        m_qT = sb([128, 2, TB], BF16, "m_qT")
        m_kT = sb([128, 2, TB], BF16, "m_kT")
        m_ks = [sb([128, 256], BF16, "m_ks%d" % i) for i in range(2)]
        m_v = [sb([128, 257], BF16, "m_v%d" % i) for i in range(2)]
        m_Dm = [sb([128, 128], F32, "m_Dm%d" % i) for i in range(2)]
        m_St = [sb([128, 128], F32, "m_St%d" % i) for i in range(2)]
        m_SD = [sb([128, 128], BF16, "m_SD%d" % i) for i in range(2)]
        m_Bs = [sb([128, 257], F32, "m_Bs%d" % i) for i in range(2)]
        m_tot = [sb([128, 257], F32, "m_tot%d" % i) for i in range(2)]
        m_ogn = sb([128, NQ, 256], F32, "m_ogn")
        m_thg = sb([128, 256], F32, "m_thg")
        m_C = sb([128, 2, 257], F32, "m_C")
        m_Cb = sb([128, 2, 257], BF16, "m_Cb")
        m_lf = sb([128, NQ], F32, "m_lf")
        m_i = sb([128, NQ], F32, "m_i")
        m_bcum = sb([128, NQ], F32, "m_bcum")
        m_bias = sb([128, NQ], F32, "m_bias")
        m_ws = sb([128, NQ], F32, "m_ws")
        m_wi = sb([128, NQ], F32, "m_wi")
        m_ecd = sb([128, NQ], F32, "m_ecd")
        m_ssq = sb([128, NQ], F32, "m_ssq")
        m_rden = sb([128, NQ], F32, "m_rden")
        m_sc = sb([128, NQ], F32, "m_sc")
        m_tmp = sb([128, NQ], F32, "m_tmp")
        m_junk = sb([128, 256], F32, "m_junk")
        m_o = [sb([128, 256], BF16, "m_o%d" % i) for i in range(2)]
        m_osem = [P.dsem() for _ in range(2)]

        def tsc(eng, out, in0, s1, s2, op0, op1, R, W):
            if s2 is None:
                P.op(eng, lambda e: e.tensor_scalar(out=out, in0=in0, scalar1=s1, scalar2=None, op0=op0),
                     reads=R, writes=W)
            else:
                P.op(eng, lambda e: e.tensor_scalar(out=out, in0=in0, scalar1=s1, scalar2=s2, op0=op0, op1=op1),
                     reads=R, writes=W)

        def stt(eng, out, in0, s, in1, op0, op1, R, W):
            P.op(eng, lambda e: e.scalar_tensor_tensor(out=out, in0=in0, scalar=s, in1=in1, op0=op0, op1=op1),
                 reads=R, writes=W)

        def tt(eng, out, in0, in1, op, R, W):
            P.op(eng, lambda e: e.tensor_tensor(out=out, in0=in0, in1=in1, op=op), reads=R, writes=W)

        def act(out, in_, func, R, W, bias=None, scale=None, accum_out=None):
            kw = {}
            if bias is not None:
                kw["bias"] = bias
            if scale is not None:
                kw["scale"] = scale
            if accum_out is not None:
                kw["accum_out"] = accum_out
            P.op("act", lambda e: e.activation(out=out, in_=in_, func=func, **kw), reads=R, writes=W)

        def mm(out, lhsT, rhs, R, W, start=True, stop=True):
            P.op("pe", lambda e: e.matmul(out, lhsT=lhsT, rhs=rhs, start=start, stop=stop), reads=R, writes=W)

        def cp(eng, out, in_, R, W):
            P.op(eng, lambda e: e.tensor_copy(out=out, in_=in_), reads=R, writes=W)

        def mset(eng, ap, v, W):
            P.op(eng, lambda e: e.memset(ap, v), writes=W)

        cops = []
        for t, d in ((cst, cst_d), (cw, cw_d), (cb, cb_d), (lp, lp_d), (mp, mp_d), (mn, mn_d), (sn, sn_d),
                     (spp, sp_d)):
            cops.append(P.dma("sp", t[:], d, csem, writes=[t]))
        for t, d in ((lw, lw_d), (mw, mw_d), (wsm, ws_d)):
            cops.append(P.dma("pool", t[:], d, csem, writes=[t]))
        for o in cops:
            o.dma_val = csem.count
        wgb = [Tile(wgb_d[g], "wgb%d" % g) for g in range(NGRP)]

        cp("dve", idb[:], IDN, [cst], [idb])
        for a, b in ((0, 6), (10, 12)):
            tsc("dve", cw[:, a:b, :], cw[:, a:b, :], 0.5, None, ALU.mult, ALU.bypass, [cw], [cw])
            tsc("dve", cb[:, a:b], cb[:, a:b], 0.5, None, ALU.mult, ALU.bypass, [cb], [cb])
        tsc("dve", lq[:, 0:4], lp[:, 0:4], 0.5, None, ALU.mult, ALU.bypass, [lp], [lq])
        act(lq[:, 4:6], lp[:, 4:6], AF.Exp, [lp], [lq], scale=-1.0)
        act(lq[:, 4:6], lq[:, 4:6], AF.Ln, [lq], [lq], bias=1.0)
        tsc("dve", lq[:, 6:8], lq[:, 4:6], -8.0, None, ALU.mult, ALU.bypass, [lq], [lq])
        tsc("dve", lq[:, 4:6], lq[:, 4:6], -4.0, None, ALU.mult, ALU.bypass, [lq], [lq])
        act(aneg[:], spp[:, 1, :], AF.Exp, [spp], [aneg])
        tsc("dve", aneg[:], aneg[:], -1.0, None, ALU.mult, ALU.bypass, [aneg], [aneg])
        for r in range(8):
            tsc("dve", dmat[:, r, :], IDN, spp[:, 2, r:r + 1], None, ALU.mult, ALU.bypass, [cst, spp], [dmat])
        mset("dve", m_v[0][:, 256:257], 1.0, [m_v[0]])
        mset("dve", m_v[1][:, 256:257], 1.0, [m_v[1]])

        wcount = [0]

        def load_group(g):
            out = []
            for i in range(2):
                k = wcount[0] % 3
                wcount[0] += 1
                P.dma("pool", wsl[k][:].rearrange("p a b -> p (a b)"),
                      wg_d[g, :, i * 16 * 256:(i + 1) * 16 * 256], wsem[k], writes=[wsl[k]])
                out.append(wsl[k])
            return out

        gcount = [0]

        for b in range(NB if stage > -2 else 0):
            for j in range(12):
                if FMp[j] is not None:
                    mset("pool", FMp[j][:, 0:3], 0.0, [FMp[j]])
            mset("pool", l_hprev[:], 0.0, [l_hprev])
            mset("pool", s_S[:], 0.0, [s_S])
            mset("pool", s_Sb[:], 0.0, [s_Sb])
            mset("pool", m_C[:], 0.0, [m_C])
            mset("pool", m_Cb[:], 0.0, [m_Cb])
            for blk in range(NBLK):
                tok0 = b * TPB + blk * TB
                for i in range(4):
                    P.dma("pool", xblk[:, i * 8:(i + 1) * 8, :].rearrange("p a b -> p (a b)"),
                          xT_d[tok0 // TB, :, i * 8 * TB:(i + 1) * 8 * TB], xsem, writes=[xblk])

                def evac_fm(j, src):
                    if j in (8, 9):
                        h = j - 8
                        act(l_sg[h][:], src, AF.Tanh, [srcT], [l_sg[h]], scale=0.5)
                        stt("dve", l_sg[h][:], l_sg[h][:], 1.0, src, ALU.add, ALU.mult, [l_sg[h], srcT], [l_sg[h]])
                    else:
                        act(FMp[j][:, 3:3 + TB], src, AF.Copy, [srcT], [FMp[j]])
                        if j in (10, 11):
                            act(m_xmT[:, j - 10, :], src, AF.Copy, [srcT], [m_xmT])

                for kind, gi in (GORDER if stage > -1 else GORDER[:NGTEST]):
                    if kind == "fm":
                        w = load_group(gi)
                        k = gcount[0] % 2
                        gcount[0] += 1
                        for hf in range(2):
                            for jj, j in enumerate(FM_GROUPS[gi]):
                                acc = pj[2 * k + jj]
                                for d in range(16):
                                    dc = hf * 16 + d
                                    mm(acc[:], w[hf][:, d, jj * 128:(jj + 1) * 128], xblk[:, dc, :], [w[hf], xblk],
                                       [acc], start=(dc == 0), stop=(dc == NDC - 1))
                        for jj, j in enumerate(FM_GROUPS[gi]):
                            srcT = pj[2 * k + jj]
                            evac_fm(j, srcT[:])
                    elif kind == "tm":
                        w = load_group(6 + gi)
                        k = gcount[0] % 2
                        gcount[0] += 1
                        for q in range(NQ):
                            acc = pj[2 * k + q // 2]
                            for hf in range(2):
                                for d in range(16):
                                    dc = hf * 16 + d
                                    mm(acc[:, (q % 2) * 256:(q % 2 + 1) * 256], xblk[:, dc, q * 128:(q + 1) * 128],
                                       w[hf][:, d, :], [w[hf], xblk], [acc], start=(dc == 0), stop=(dc == NDC - 1))
                        for q in range(NQ):
                            srcT = pj[2 * k + q // 2]
                            src = srcT[:, (q % 2) * 256:(q % 2 + 1) * 256]
                            if gi in (0, 1):
                                dst = s_zs[:, q, gi * 256:(gi + 1) * 256]
                                act(dst, src, AF.Tanh, [srcT], [s_zs], scale=0.5)
                                stt("dve", dst, dst, 1.0, src, ALU.add, ALU.mult, [s_zs, srcT], [s_zs])
                            elif gi == 2:
                                act(m_ogn[:, q, :], src, AF.Tanh, [srcT], [m_ogn], scale=0.5)
                            else:
                                act(m_thg[:], src, AF.Tanh, [srcT], [m_thg], scale=0.5)
                                stt("dve", m_thg[:], m_thg[:], 1.0, src, ALU.add, ALU.mult, [m_thg, srcT], [m_thg])
                                stt("dve", m_ogn[:, q, :], m_ogn[:, q, :], 1.0, m_thg[:], ALU.add, ALU.mult,
                                    [m_ogn, m_thg], [m_ogn])
                                tt("pool", m_ogn[:, q, :], m_ogn[:, q, :], mn[:], ALU.mult, [m_ogn, mn], [m_ogn])
                    else:
                        for q in range(NQ):
                            for dc in range(NDC):
                                mm(pS[:, q * 16:q * 16 + 10], xblk[:, dc, q * 128:(q + 1) * 128], wsm[:, dc, :],
                                   [wsm, xblk], [pS], start=(dc == 0), stop=(dc == NDC - 1))
                        for q in range(NQ):
                            cp("dve", s_sm[:, q, :], pS[:, q * 16:q * 16 + 10], [pS], [s_sm])

                if stage < 1:
                    continue
                for h in range(2):
                    j = 6 + h
                    tsc("dve", l_xc[:], FMp[j][:, 3:3 + TB], cw[:, j, 3:4], cb[:, j:j + 1], ALU.mult, ALU.add,
                        [FMp[j], cw, cb], [l_xc])
                    for kk in range(1, 4):
                        stt("dve", l_xc[:], FMp[j][:, 3 - kk:3 - kk + TB], cw[:, j, 3 - kk:4 - kk], l_xc[:],
                            ALU.mult, ALU.add, [FMp[j], cw, l_xc], [l_xc])
                    cp("pool", FMp[j][:, 0:3], FMp[j][:, TB:TB + 3], [FMp[j]], [FMp[j]])
                    act(l_xcb[:], l_xc[:], AF.Copy, [l_xc], [l_xcb])
                    pa, px = nextpm(), nextpm()
                    mm(pa[:], lw[:, h, :], l_xcb[:], [lw, l_xcb], [pa])
                    mm(px[:], lw[:, 2 + h, :], l_xcb[:], [lw, l_xcb], [px])
                    act(l_a[:], pa[:], AF.Tanh, [pa, lq], [l_a], scale=0.5, bias=lq[:, h:h + 1])
                    act(l_m[:], l_a[:], AF.Exp, [l_a, lq], [l_m], scale=lq[:, 6 + h:7 + h], bias=lq[:, 6 + h:7 + h])
                    act(l_a[:], l_a[:], AF.Exp, [l_a, lq], [l_a], scale=lq[:, 4 + h:5 + h], bias=lq[:, 4 + h:5 + h])
                    tsc("dve", l_m[:], l_m[:], -1.0, 1.0, ALU.mult, ALU.add, [l_m], [l_m])
                    act(l_m[:], l_m[:], AF.Sqrt, [l_m], [l_m])
                    act(l_u[:], px[:], AF.Tanh, [px, lq], [l_u], scale=0.5, bias=lq[:, 2 + h:3 + h])
                    stt("dve", l_u[:], l_u[:], 1.0, l_xc[:], ALU.add, ALU.mult, [l_u, l_xc], [l_u])
                    stt("dve", l_u[:], l_u[:], 0.5, l_m[:], ALU.mult, ALU.mult, [l_u, l_m], [l_u])
                    hh = l_h[h]
                    P.op("dve", lambda e, hh=hh, h=h: e.tensor_tensor_scan(
                        out=hh[:], data0=l_a[:], data1=l_u[:], initial=l_hprev[:, h:h + 1], op0=ALU.mult,
                        op1=ALU.add), reads=[l_a, l_u, l_hprev], writes=[hh])
                    cp("pool", l_hprev[:, h:h + 1], hh[:, TB - 1:TB], [hh], [l_hprev])
                    stt("dve", l_y[h][:], hh[:], 0.5, l_sg[h][:], ALU.mult, ALU.mult, [hh, l_sg[h]], [l_y[h]])
                    P.dma("sp", ylru_d[h * 128:(h + 1) * 128, tok0:tok0 + TB], l_y[h][:], l_ysem[h], reads=[l_y[h]])

                if stage < 2:
                    continue
                for j in range(6):
                    t, th = s_t[j % 2], s_th[j % 2]
                    eng = "dve"
                    tsc(eng, t[:], FMp[j][:, 3:3 + TB], cw[:, j, 3:4], cb[:, j:j + 1], ALU.mult, ALU.add,
                        [FMp[j], cw, cb], [t])
                    for kk in range(1, 4):
                        stt(eng, t[:], FMp[j][:, 3 - kk:3 - kk + TB], cw[:, j, 3 - kk:4 - kk], t[:],
                            ALU.mult, ALU.add, [FMp[j], cw, t], [t])
                    cp("pool", FMp[j][:, 0:3], FMp[j][:, TB:TB + 3], [FMp[j]], [FMp[j]])
                    act(th[:], t[:], AF.Tanh, [t], [th])
                    if j < 4:
                        dstT, dst = s_xsT, s_xsT[:, j, :]
                    elif j == 4:
                        dstT, dst = s_BT, s_BT[:]
                    else:
                        dstT, dst = s_CT, s_CT[:]
                    stt("dve", dst, th[:], 1.0, t[:], ALU.add, ALU.mult, [th, t], [dstT])
                tt("dve", s_dt[:], s_sm[:, :, 0:8], spp[:, 0:1, :].to_broadcast([128, NQ, 8]), ALU.add,
                   [s_sm, spp], [s_dt])
                act(s_dt[:], s_dt[:], AF.Exp, [s_dt], [s_dt])
                act(s_dt[:], s_dt[:], AF.Ln, [s_dt], [s_dt], bias=1.0)
                tt("dve", s_a[:], s_dt[:], aneg[:].unsqueeze(1).to_broadcast([128, NQ, 8]), ALU.mult,
                   [s_dt, aneg], [s_a])
                pc = nextpm()
                a2d = s_a[:].rearrange("p q r -> p (q r)")
                mm(pc[:, 0:32], U, a2d, [cst, s_a], [pc])
                mm(pc[:, 32:64], ONES, a2d, [cst, s_a], [pc])
                cp("dve", s_acum[:].rearrange("p q r -> p (q r)"), pc[:, 0:32], [pc], [s_acum])
                tsc("dve", s_nacum[:].rearrange("p q r -> p (q r)"), pc[:, 0:32], -1.0, None, ALU.mult, ALU.bypass,
                    [pc], [s_nacum])
                act(s_eacum[:].rearrange("p q r -> p (q r)"), pc[:, 0:32], AF.Exp, [pc], [s_eacum])
                act(s_ecd[:].rearrange("p q r -> p (q r)"), pc[:, 32:64], AF.Exp, [pc], [s_ecd])
                tt("dve", s_ds[:].rearrange("p q r -> p (q r)"), pc[:, 32:64],
                   s_acum[:].rearrange("p q r -> p (q r)"), ALU.subtract, [pc, s_acum], [s_ds])
                act(s_ds[:], s_ds[:], AF.Exp, [s_ds], [s_ds])
                tt("dve", s_ds[:], s_ds[:], s_dt[:], ALU.mult, [s_ds, s_dt], [s_ds])
                for q in range(NQ):
                    qs = slice(q * 128, (q + 1) * 128)
                    tokT = s_tok[q % 2]
                    ptr = nextpm()
                    ptb = ptr[:].bitcast(BF16)
                    for j in range(4):
                        P.op("pe", lambda e, j=j, ptb=ptb, qs=qs: e.transpose(ptb[:, j * 128:(j + 1) * 128],
                                                                            s_xsT[:, j, qs], idb[:]),
                             reads=[s_xsT, idb], writes=[ptr])
                    P.op("pe", lambda e, ptb=ptb, qs=qs: e.transpose(ptb[:, 512:640], s_BT[:, qs], idb[:]),
                         reads=[s_BT, idb], writes=[ptr])
                    act(tokT[:], ptb[:, 0:640], AF.Copy, [ptr], [tokT])
                    pg = nextpm()
                    mm(pg[:, 0:128], s_BT[:, qs], s_CT[:, qs], [s_BT, s_CT], [pg])
                    G = s_G[q % 2]
                    act(G[:], pg[:, 0:128], AF.Copy, [pg], [G])
                    for half in range(2):
                        pr = nextpm()
                        for rr in range(4):
                            r = half * 4 + rr
                            mm(pr[:, rr * 128:(rr + 1) * 128], IDN, MNEG, [cst], [pr], start=True, stop=False)
                            mm(pr[:, rr * 128:(rr + 1) * 128], s_a[:, q, r:r + 1].to_broadcast([128, 128]), U,
                               [s_a, cst], [pr], start=False, stop=True)
                        for rr in range(4):
                            r = half * 4 + rr
                            E = s_E[rr]
                            act(E[:], pr[:, rr * 128:(rr + 1) * 128], AF.Exp, [pr, s_nacum], [E],
                                bias=s_nacum[:, q, r:r + 1])
                            stt("dve", s_W[r][:], E[:], s_dt[:, q, r:r + 1], G[:], ALU.mult, ALU.mult,
                                [E, s_dt, G], [s_W[r]])
                    py = nextpm()
                    for r in range(8):
                        rs_ = slice(r * 64, (r + 1) * 64)
                        mm(py[:, rs_], s_W[r][:], tokT[:, rs_], [s_W[r], tokT], [py], start=True, stop=False)
                        mm(py[:, rs_], dmat[:, r, :], tokT[:, rs_], [dmat, tokT], [py], start=False, stop=True)
                    po = nextpm()
                    mm(po[:], s_CT[:, qs], s_Sb[:], [s_CT, s_Sb], [po])
                    y1 = s_y1[q % 2]
                    tt("dve", y1[:].rearrange("p (r c) -> p r c", r=8), po[:].rearrange("p (r c) -> p r c", r=8),
                       s_eacum[:, q, :].unsqueeze(2).to_broadcast([128, 8, 64]), ALU.mult, [po, s_eacum], [y1])
                    tt("dve", y1[:], y1[:], py[:], ALU.add, [y1, py], [y1])
                    tt("pool", s_yg[:, q, :], y1[:], s_zs[:, q, :], ALU.mult, [y1, s_zs], [s_zs])
                    act(s_junk[:], s_yg[:, q, :], AF.Square, [s_yg], [s_junk])
                    P.op("dve", lambda e, q=q: e.reduce_sum(out=s_ssq[:, q:q + 1], in_=s_junk[:], axis=AX.X),
                         reads=[s_junk], writes=[s_ssq])
                    xsc = s_xsc[q % 2]
                    tt("pool", xsc[:].rearrange("p (r c) -> p r c", r=8),
                       tokT[:, 0:512].rearrange("p (r c) -> p r c", r=8),
                       s_ds[:, q, :].unsqueeze(2).to_broadcast([128, 8, 64]), ALU.mult, [tokT, s_ds], [xsc])
                    pst = nextpm()
                    mm(pst[:], tokT[:, 512:640], xsc[:], [tokT, xsc], [pst])
                    tt("pool", s_S[:].rearrange("p (r c) -> p r c", r=8), s_S[:].rearrange("p (r c) -> p r c", r=8),
                       s_ecd[:, q, :].unsqueeze(2).to_broadcast([128, 8, 64]), ALU.mult, [s_S, s_ecd], [s_S])
                    tt("dve", s_S[:], s_S[:], pst[:], ALU.add, [s_S, pst], [s_S])
                    act(s_Sb[:], s_S[:], AF.Copy, [s_S], [s_Sb])
                tsc("dve", s_rstd[:], s_ssq[:], 1.0 / 512, 4 * RMS_EPS, ALU.mult, ALU.add, [s_ssq], [s_rstd])
                act(s_rstd[:], s_rstd[:], AF.Sqrt, [s_rstd], [s_rstd])
                P.op("dve", lambda e: e.reciprocal(out=s_rstd[:], in_=s_rstd[:]), reads=[s_rstd], writes=[s_rstd])
                for q in range(NQ):
                    o = s_o[q % 2]
                    stt("dve", o[:], s_yg[:, q, :], s_rstd[:, q:q + 1], sn[:], ALU.mult, ALU.mult,
                        [s_yg, s_rstd, sn], [o])
                    P.dma("sp", yssd_d[tok0 + q * 128:tok0 + (q + 1) * 128, :], o[:], s_osem[q % 2], reads=[o])

                if stage < 3:
                    continue
                for ic in range(2):
                    j = 10 + ic
                    eng = "dve"
                    mt, mth = s_t[ic], s_th[ic]
                    tsc(eng, mt[:], FMp[j][:, 3:3 + TB], cw[:, j, 3:4], cb[:, j:j + 1], ALU.mult, ALU.add,
                        [FMp[j], cw, cb], [mt])
                    for kk in range(1, 4):
                        stt(eng, mt[:], FMp[j][:, 3 - kk:3 - kk + TB], cw[:, j, 3 - kk:4 - kk], mt[:],
                            ALU.mult, ALU.add, [FMp[j], cw, mt], [mt])
                    cp("pool", FMp[j][:, 0:3], FMp[j][:, TB:TB + 3], [FMp[j]], [FMp[j]])
                    act(mth[:], mt[:], AF.Tanh, [mt], [mth])
                    stt("dve", m_xcT[:, ic, :], mth[:], 1.0, mt[:], ALU.add, ALU.mult, [mth, mt], [m_xcT])
                for wi, dstT, scl in ((0, m_qT, 1.0), (2, m_kT, 1.0 / 16)):
                    for jc in range(2):
                        pq = nextpm()
                        for ic in range(2):
                            mm(pq[:], mw[:, wi + ic, jc * 128:(jc + 1) * 128], m_xcT[:, ic, :], [mw, m_xcT], [pq],
                               start=(ic == 0), stop=(ic == 1))
                        act(dstT[:, jc, :], pq[:], AF.Copy, [pq], [dstT], scale=scl)
                tsc("dve", m_i[:], s_sm[:, :, 8], mp[:, 0:1], None, ALU.add, ALU.bypass, [s_sm, mp], [m_i])
                tsc("dve", m_lf[:], s_sm[:, :, 9], mp[:, 1:2], -1.0, ALU.add, ALU.mult, [s_sm, mp], [m_lf])
                act(m_lf[:], m_lf[:], AF.Exp, [m_lf], [m_lf])
                act(m_lf[:], m_lf[:], AF.Ln, [m_lf], [m_lf], bias=1.0)
                tsc("dve", m_lf[:], m_lf[:], -1.0, None, ALU.mult, ALU.bypass, [m_lf], [m_lf])
                pc = nextpm()
                mm(pc[:, 0:NQ], U, m_lf[:], [cst, m_lf], [pc])
                mm(pc[:, 32:32 + NQ], ONES, m_lf[:], [cst, m_lf], [pc])
                cp("dve", m_bcum[:], pc[:, 0:NQ], [pc], [m_bcum])
                tt("dve", m_bias[:], m_i[:], m_bcum[:], ALU.subtract, [m_i, m_bcum], [m_bias])
                tt("dve", m_ws[:], pc[:, 32:32 + NQ], m_bias[:], ALU.add, [pc, m_bias], [m_ws])
                act(m_ws[:], m_ws[:], AF.Exp, [m_ws], [m_ws])
                tsc("dve", m_ws[:], m_ws[:], 1.0 / 16, None, ALU.mult, ALU.bypass, [m_ws], [m_ws])
                act(m_wi[:], m_bcum[:], AF.Exp, [m_bcum], [m_wi])
                act(m_ecd[:], pc[:, 32:32 + NQ], AF.Exp, [pc], [m_ecd])
                for q in range(NQ):
                    qs = slice(q * 128, (q + 1) * 128)
                    ks, v = m_ks[q % 2], m_v[q % 2]
                    pk = nextpm()
                    for ic in range(2):
                        mm(pk[:, 0:256], m_xcT[:, ic, qs], mw[:, 2 + ic, :], [m_xcT, mw], [pk], start=(ic == 0),
                           stop=(ic == 1))
                    for ic in range(2):
                        mm(pk[:, 256:512], m_xmT[:, ic, qs], mw[:, 4 + ic, :], [m_xmT, mw], [pk], start=(ic == 0),
                           stop=(ic == 1))
                    act(ks[:], pk[:, 0:256], AF.Copy, [pk, m_ws], [ks], scale=m_ws[:, q:q + 1])
                    act(v[:, 0:256], pk[:, 256:512], AF.Copy, [pk], [v])
                    pd = nextpm()
                    mm(pd[:, 0:128], IDN, MNEG, [cst], [pd], start=True, stop=False)
                    mm(pd[:, 0:128], m_lf[:, q:q + 1].to_broadcast([128, 128]), U, [m_lf, cst], [pd], start=False,
                       stop=True)
                    for jc in range(2):
                        mm(pd[:, 128:256], m_kT[:, jc, qs], m_qT[:, jc, qs], [m_kT, m_qT], [pd], start=(jc == 0),
                           stop=(jc == 1))
                    Dm, SD = m_Dm[q % 2], m_SD[q % 2]
                    act(Dm[:], pd[:, 0:128], AF.Exp, [pd, m_bias], [Dm], bias=m_bias[:, q:q + 1])
                    St = m_St[q % 2]
                    act(St[:], pd[:, 128:256], AF.Copy, [pd], [St])
                    tt("dve", SD[:], St[:], Dm[:], ALU.mult, [St, Dm], [SD])
                    pA, pB = nextpm(), nextpm()
                    mm(pA[:, 0:257], SD[:], v[:], [SD, v], [pA])
                    for dc in range(2):
                        mm(pB[:, 0:257], m_qT[:, dc, qs], m_Cb[:, dc, :], [m_qT, m_Cb], [pB], start=(dc == 0),
                           stop=(dc == 1))
                    Bs, tot = m_Bs[q % 2], m_tot[q % 2]
                    act(Bs[:], pB[:, 0:257], AF.Copy, [pB, m_wi], [Bs], scale=m_wi[:, q:q + 1])
                    tt("dve", tot[:], Bs[:], pA[:, 0:257], ALU.add, [Bs, pA], [tot])
                    stt("dve", m_rden[:, q:q + 1], tot[:, 256:257], 1.0, tot[:, 256:257], ALU.mult, ALU.mult,
                        [tot], [m_rden])
                    tsc("dve", m_rden[:, q:q + 1], m_rden[:, q:q + 1], 1.0, None, ALU.max, ALU.bypass, [m_rden],
                        [m_rden])
                    tt("pool", m_ogn[:, q, :], m_ogn[:, q, :], tot[:, 0:256], ALU.mult, [m_ogn, tot], [m_ogn])
                    act(m_junk[:], tot[:, 0:256], AF.Square, [tot], [m_junk])
                    P.op("dve", lambda e, q=q: e.reduce_sum(out=m_ssq[:, q:q + 1], in_=m_junk[:], axis=AX.X),
                         reads=[m_junk], writes=[m_ssq])
                    pcu = nextpm()
                    for dc in range(2):
                        mm(pcu[:, dc * 256:(dc + 1) * 256], ks[:, dc * 128:(dc + 1) * 128], v[:, 0:256], [ks, v], [pcu])
                    pn = nextpm()
                    for dc in range(2):
                        mm(pn[:, dc:dc + 1], ks[:, dc * 128:(dc + 1) * 128], v[:, 256:257], [ks, v], [pn])
                    act(m_C[:], m_C[:], AF.Copy, [m_C, m_ecd], [m_C], scale=m_ecd[:, q:q + 1])
                    tt("dve", m_C[:, :, 0:256], m_C[:, :, 0:256], pcu[:].rearrange("p (a b) -> p a b", a=2), ALU.add,
                       [m_C, pcu], [m_C])
                    tt("dve", m_C[:, :, 256:257], m_C[:, :, 256:257], pn[:, 0:2].unsqueeze(2), ALU.add,
                       [m_C, pn], [m_C])
                    act(m_Cb[:], m_C[:], AF.Copy, [m_C], [m_Cb])
                tsc("dve", m_tmp[:], m_rden[:], RMS_EPS, None, ALU.mult, ALU.bypass, [m_rden], [m_tmp])
                stt("dve", m_tmp[:], m_ssq[:], 1.0 / 256, m_tmp[:], ALU.mult, ALU.add, [m_ssq, m_tmp], [m_tmp])
                act(m_tmp[:], m_tmp[:], AF.Sqrt, [m_tmp], [m_tmp])
                P.op("dve", lambda e: e.reciprocal(out=m_sc[:], in_=m_tmp[:]), reads=[m_tmp], writes=[m_sc])
                tsc("dve", m_sc[:], m_sc[:], 0.25, None, ALU.mult, ALU.bypass, [m_sc], [m_sc])
                for q in range(NQ):
                    o = m_o[q % 2]
                    tsc("dve", o[:], m_ogn[:, q, :], m_sc[:, q:q + 1], None, ALU.mult, ALU.bypass, [m_ogn, m_sc], [o])
                    P.dma("sp", yml_d[tok0 + q * 128:tok0 + (q + 1) * 128, :], o[:], m_osem[q % 2], reads=[o])
        P.emit(final_waits=l_ysem + s_osem + m_osem)
    return nc


OFF = dict(z=0, xbc=4096, dt=10240, lru_x=10304, lru_g=12352, ml_x=14400, ml_o=16448, ml_g=18496, ml_if=20544)


def fm_cols(c, j):
    if j < 4:
        s0 = OFF["xbc"] + c * 512 + j * 128
    elif j == 4:
        s0 = OFF["xbc"] + 4096 + c * 128
    elif j == 5:
        s0 = OFF["xbc"] + 4096 + 1024 + c * 128
    elif j < 8:
        s0 = OFF["lru_x"] + c * 256 + (j - 6) * 128
    elif j < 10:
        s0 = OFF["lru_g"] + c * 256 + (j - 8) * 128
    else:
        s0 = OFF["ml_x"] + c * 256 + (j - 10) * 128
    return np.arange(s0, s0 + 128)


def host_consts():
    t = np.arange(128)
    U = (t[:, None] <= t[None, :]).astype(np.float32)
    ones = np.ones((128, 128), np.float32)
    mneg = np.where(t[:, None] > t[None, :], -30000.0, 0.0).astype(np.float32)
    idn = np.eye(128, dtype=np.float32)
    return np.ascontiguousarray(np.stack([U, ones, mneg, idn], axis=1))


def host_layout_a(inp, l, c):
    w = inp["w_in"][l]
    groups = []
    for gi in range(6):
        groups.append(np.concatenate([fm_cols(c, j) for j in FM_GROUPS[gi]]))
    groups.append(np.arange(OFF["z"] + c * 512, OFF["z"] + c * 512 + 256))
    groups.append(np.arange(OFF["z"] + c * 512 + 256, OFF["z"] + c * 512 + 512))
    groups.append(np.arange(OFF["ml_o"] + c * 256, OFF["ml_o"] + (c + 1) * 256))
    groups.append(np.arange(OFF["ml_g"] + c * 256, OFF["ml_g"] + (c + 1) * 256))
    wg = np.stack([w[:, g].reshape(32, 128, 256).transpose(1, 0, 2).reshape(128, 32 * 256) for g in groups], axis=0)
    smc = np.concatenate([np.arange(OFF["dt"] + c * 8, OFF["dt"] + c * 8 + 8), [OFF["ml_if"] + c],
                          [OFF["ml_if"] + 8 + c]])
    ws = np.ascontiguousarray(w[:, smc].reshape(32, 128, 10).transpose(1, 0, 2))
    cw = np.zeros((128, 12, 4), np.float32)
    cb = np.zeros((128, 12), np.float32)
    for j in range(12):
        if j < 6:
            ch = fm_cols(c, j) - OFF["xbc"]
            cw[:, j, :] = inp["ssd_conv_w"][l][:, ch].T
            cb[:, j] = inp["ssd_conv_b"][l][ch]
        elif j < 8:
            ch = fm_cols(c, j) - OFF["lru_x"]
            cw[:, j, :] = inp["lru_conv_w"][l][:, ch].T
            cb[:, j] = inp["lru_conv_b"][l][ch]
        elif j >= 10:
            ch = fm_cols(c, j) - OFF["ml_x"]
            cw[:, j, :] = inp["mlstm_conv_w"][l][:, ch].T
            cb[:, j] = inp["mlstm_conv_b"][l][ch]
    lw = np.stack([inp["lru_w_a"][l][2 * c], inp["lru_w_a"][l][2 * c + 1], inp["lru_w_x"][l][2 * c],
                   inp["lru_w_x"][l][2 * c + 1]], axis=1)
    lp = np.zeros((128, 6), np.float32)
    for h in range(2):
        sl = slice(c * 256 + h * 128, c * 256 + (h + 1) * 128)
        lp[:, h] = inp["lru_b_a"][l][sl]
        lp[:, 2 + h] = inp["lru_b_x"][l][sl]
        lp[:, 4 + h] = inp["lru_lambda"][l][sl]
    mw = np.stack([inp[k][l][c][ic * 128:(ic + 1) * 128, :] for k in ("mlstm_w_q", "mlstm_w_k", "mlstm_w_v")
                   for ic in range(2)], axis=1)
    mp = np.zeros((128, 2), np.float32)
    mp[:, 0] = inp["mlstm_b_i"][l][c]
    mp[:, 1] = inp["mlstm_b_f"][l][c]
    mn = np.tile(inp["mlstm_norm_w"][l][c * 256:(c + 1) * 256][None, :], (128, 1))
    sn = np.tile(inp["ssd_norm_w"][l][c * 512:(c + 1) * 512][None, :], (128, 1))
    sp = np.stack([np.tile(inp[k][l][c * 8:(c + 1) * 8][None, :], (128, 1))
                   for k in ("ssd_dt_bias", "ssd_a_log", "ssd_d")], axis=1)
    f = lambda a: np.ascontiguousarray(a, dtype=np.float32)
    return {"wg": f(wg), "ws": f(ws), "cst": host_consts(), "cw": f(cw), "cb": f(cb), "lw": f(lw), "lp": f(lp),
            "mw": f(mw), "mp": f(mp), "mn": f(mn), "sn": f(sn), "sp": f(sp)}


def host_x_blocks(xT_tokmajor):
    nt = xT_tokmajor.shape[0]
    a = xT_tokmajor.reshape(nt // TB, TB, NDC, 128)
    return np.ascontiguousarray(a.transpose(0, 3, 2, 1)).reshape(nt // TB, 128, NDC * TB)


MIXW = 8192
ALPHA = (2.0 * 2) ** 0.25
LN_EPS = 1e-5


def build_b(Tc):
    nc = bass.Bass("TRN2", target_bir_lowering=False)
    NDC = D_MODEL // 128
    NFC = MIXW // 128
    TH = Tc // 512
    yT_d = nc.dram_tensor("yT", [MIXW, Tc], BF16, kind="ExternalInput").ap()
    xT_d = nc.dram_tensor("xT", [D_MODEL, Tc], F32, kind="ExternalInput").ap()
    w_d = nc.dram_tensor("w_r", [NDC, 128, NFC * 128], F32, kind="ExternalInput").ap()
    lnw_d = nc.dram_tensor("lnw", [128, NDC], F32, kind="ExternalInput").ap()
    lnb_d = nc.dram_tensor("lnb", [128, NDC], F32, kind="ExternalInput").ap()
    oT_d = nc.dram_tensor("oT", [D_MODEL, Tc], F32, kind="ExternalOutput").ap()
    hs_d = nc.dram_tensor("hscr", [NDC, 128, Tc], F32, kind="Internal").ap()
    with ExitStack() as st:
        P = Prog(nc, st)
        yT = P.sbuf([128, NFC, Tc], BF16, "yT_sb")
        wb = [P.sbuf([128, NFC, 128], BF16, "wb%d" % i) for i in range(2)]
        wsem = [P.dsem() for _ in range(2)]
        xb = [P.sbuf([128, 512], F32, "xb%d" % i) for i in range(3)]
        xsem = [P.dsem() for _ in range(3)]
        hb = [P.sbuf([128, 512], F32, "hb%d" % i) for i in range(3)]
        h2b = [P.sbuf([128, 512], F32, "h2b%d" % i) for i in range(2)]
        hsem = [P.dsem() for _ in range(3)]
        onesD = P.sbuf([128, 128], F32, "onesD")
        lnw = P.sbuf([128, NDC], F32, "lnw_sb")
        lnb = P.sbuf([128, NDC], F32, "lnb_sb")
        meanb = P.sbuf([128, Tc], F32, "meanb")
        rstdb = P.sbuf([128, Tc], F32, "rstdb")
        m2 = P.sbuf([128, Tc], F32, "m2")
        hl = [P.sbuf([128, 512], F32, "hl%d" % i) for i in range(2)]
        hlsem = [P.dsem() for _ in range(2)]
        ob = [P.sbuf([128, 512], F32, "ob%d" % i) for i in range(2)]
        osem = [P.dsem() for _ in range(2)]
        acc = [P.psum([128, 512], F32, "acc%d" % i) for i in range(2)]
        S1 = [P.psum([128, 512], F32, "S1_%d" % i) for i in range(TH)]
        S2 = [P.psum([128, 512], F32, "S2_%d" % i) for i in range(TH)]
        hscr = [[Tile(hs_d[dc, :, th * 512:(th + 1) * 512], "hs%d_%d" % (dc, th)) for th in range(TH)]
                for dc in range(NDC)]
        csem = P.dsem()
        ysem = P.dsem()

        cops = [P.dma("sp", lnw[:], lnw_d, csem, writes=[lnw]), P.dma("sp", lnb[:], lnb_d, csem, writes=[lnb])]
        for o in cops:
            o.dma_val = csem.count
        yv = yT_d.rearrange("(fc p) t -> p fc t", p=128)
        NY = 8
        for i in range(NY):
            a, b = i * NFC // NY, (i + 1) * NFC // NY
            P.dma("sp", yT[:, a:b, :], yv[:, a:b, :], ysem, writes=[yT])
        P.op("dve", lambda e: e.memset(onesD[:], 1.0 / D_MODEL), writes=[onesD])

        pending = []
        step = 0
        for dc in range(NDC):
            w = wb[dc % 2]
            P.dma("pool", w[:].rearrange("p a b -> p (a b)"), w_d[dc], wsem[dc % 2], writes=[w])
            for th in range(TH):
                a = acc[step % 2]
                ts = slice(th * 512, (th + 1) * 512)
                x = xb[step % 3]
                P.dma("sp", x[:], xT_d[dc * 128:(dc + 1) * 128, ts], xsem[step % 3], writes=[x])
                for fc in range(NFC):
                    P.op("pe", lambda e, a=a, w=w, fc=fc, ts=ts: e.matmul(
                        a[:], lhsT=w[:, fc, :], rhs=yT[:, fc, ts], start=(fc == 0), stop=(fc == NFC - 1)),
                        reads=[w, yT], writes=[a])
                for f in pending:
                    f()
                pending = []
                h = hb[step % 3]
                h2 = h2b[step % 2]
                P.op("dve", lambda e, h=h, x=x, a=a: e.scalar_tensor_tensor(
                    out=h[:], in0=x[:], scalar=ALPHA, in1=a[:], op0=ALU.mult, op1=ALU.add),
                    reads=[x, a], writes=[h])
                P.op("act", lambda e, h=h, h2=h2: e.activation(out=h2[:], in_=h[:], func=AF.Square),
                     reads=[h], writes=[h2])

                def stats(h=h, h2=h2, th=th, dc=dc):
                    P.op("pe", lambda e: e.matmul(S1[th][:], lhsT=onesD[:], rhs=h[:], start=(dc == 0),
                                                  stop=(dc == NDC - 1)), reads=[onesD, h], writes=[S1[th]])
                    P.op("pe", lambda e: e.matmul(S2[th][:], lhsT=onesD[:], rhs=h2[:], start=(dc == 0),
                                                  stop=(dc == NDC - 1)), reads=[onesD, h2], writes=[S2[th]])
                pending.append(stats)
                P.dma("sp", hscr[dc][th][:], h[:], hsem[step % 3], reads=[h], writes=[hscr[dc][th]])
                step += 1
        for f in pending:
            f()
        for th in range(TH):
            ts = slice(th * 512, (th + 1) * 512)
            P.op("act", lambda e, th=th, ts=ts: e.activation(out=meanb[:, ts], in_=S1[th][:], func=AF.Copy),
                 reads=[S1[th]], writes=[meanb])
        P.op("dve", lambda e: e.tensor_tensor(out=m2[:], in0=meanb[:], in1=meanb[:], op=ALU.mult),
             reads=[meanb], writes=[m2])
        for th in range(TH):
            ts = slice(th * 512, (th + 1) * 512)
            P.op("dve", lambda e, th=th, ts=ts: e.tensor_tensor(out=rstdb[:, ts], in0=S2[th][:], in1=m2[:, ts],
                                                                op=ALU.subtract),
                 reads=[S2[th], m2], writes=[rstdb])
        P.op("dve", lambda e: e.tensor_scalar_add(out=rstdb[:], in0=rstdb[:], scalar1=LN_EPS),
             reads=[rstdb], writes=[rstdb])
        P.op("act", lambda e: e.activation(out=m2[:], in_=rstdb[:], func=AF.Sqrt), reads=[rstdb], writes=[m2])
        P.op("dve", lambda e: e.reciprocal(out=rstdb[:], in_=m2[:]), reads=[m2], writes=[rstdb])
        step = 0
        for dc in range(NDC):
            for th in range(TH):
                ts = slice(th * 512, (th + 1) * 512)
                t = hl[step % 2]
                o = ob[step % 2]
                P.dma("sp", t[:], hscr[dc][th][:], hlsem[step % 2], reads=[hscr[dc][th]], writes=[t])
                P.op("dve", lambda e, t=t, ts=ts: e.tensor_tensor(out=t[:], in0=t[:], in1=meanb[:, ts],
                                                                  op=ALU.subtract), reads=[t, meanb], writes=[t])
                P.op("pool", lambda e, t=t, ts=ts: e.tensor_tensor(out=t[:], in0=t[:], in1=rstdb[:, ts],
                                                                   op=ALU.mult), reads=[t, rstdb], writes=[t])
                P.op("act", lambda e, t=t, o=o, dc=dc: e.activation(out=o[:], in_=t[:], func=AF.Identity,
                                                                    scale=lnw[:, dc:dc + 1], bias=lnb[:, dc:dc + 1]),
                     reads=[t, lnw, lnb], writes=[o])
                P.dma("sp", oT_d[dc * 128:(dc + 1) * 128, ts], o[:], osem[step % 2], reads=[o])
                step += 1
        P.emit(final_waits=osem)
    return nc


def host_layout_b(w_out_l, ln_w_l, ln_b_l):
    NDC, NFC = D_MODEL // 128, MIXW // 128
    w_r = np.ascontiguousarray(
        w_out_l.reshape(NFC, 128, NDC, 128).transpose(2, 1, 0, 3)).reshape(NDC, 128, NFC * 128)
    lnw = np.ascontiguousarray(ln_w_l.reshape(NDC, 128).T)
    lnb = np.ascontiguousarray(ln_b_l.reshape(NDC, 128).T)
    return w_r, lnw, lnb


import ml_dtypes
from concourse.bass_utils import run_bass_kernel_spmd

NCORES = 8
BATCH, SEQ, DEPTH = 2, 4096, 2
_CACHE = {}


def _get(name, fn):
    if name not in _CACHE:
        _CACHE[name] = fn()
    return _CACHE[name]


def kernel(**inp):
    inp = {k: np.asarray(v) for k, v in inp.items()}
    NTOK = BATCH * SEQ
    Tc = NTOK // NCORES
    x_tok = inp["x"].reshape(NTOK, D_MODEL)
    xT = np.ascontiguousarray(x_tok.T)
    for l in range(DEPTH):
        nca = build_a(BATCH, SEQ)
        xb = host_x_blocks(np.ascontiguousarray(xT.T))
        ims = []
        for c in range(NCORES):
            m = host_layout_a(inp, l, c)
            m["xT"] = xb
            ims.append(m)
        ra = run_bass_kernel_spmd(nca, ims, core_ids=list(range(NCORES))).results
        yT = np.empty((8192, NTOK), dtype=ml_dtypes.bfloat16)
        for c in range(NCORES):
            yT[c * 512:(c + 1) * 512, :] = ra[c]["yssd"].T
            yT[4096 + c * 256:4096 + (c + 1) * 256, :] = ra[c]["ylru"]
            yT[6144 + c * 256:6144 + (c + 1) * 256, :] = ra[c]["yml"].T
        ncb = build_b(Tc)
        w_r, lnw, lnb = host_layout_b(inp["w_out"][l], inp["ln_w"][l], inp["ln_b"][l])
        ims = []
        for c in range(NCORES):
            ts = slice(c * Tc, (c + 1) * Tc)
            ims.append({"yT": np.ascontiguousarray(yT[:, ts]), "xT": np.ascontiguousarray(xT[:, ts]),
                        "w_r": w_r, "lnw": lnw, "lnb": lnb})
        rb = run_bass_kernel_spmd(ncb, ims, core_ids=list(range(NCORES))).results
        xT = np.concatenate([rb[c]["oT"] for c in range(NCORES)], axis=1)
    return np.ascontiguousarray(xT.T).reshape(BATCH, SEQ, D_MODEL).astype(np.float32)
```
